# Optimizing a Trainium2 kernel written in Bass

```python
import math
import jax, jax.numpy as jnp
from jax import lax
import numpy as np

D_MODEL = 1024
BATCH = 8
SEQ = 2048
DEPTH = 4
DEC_BATCH = 128
DEC_SEQ = 4
PAST_LEN = 16384
PAGE_SIZE = 128

N_MIXERS = 2
N_GDN = (DEPTH + 1) // 2
N_SSD = DEPTH // 2
CONV_W = 4
NORM_EPS = 1e-6
GDN_HEADS = 8
GDN_DK = 128
GDN_DV = 128
GDN_QK = GDN_HEADS * GDN_DK
GDN_VW = GDN_HEADS * GDN_DV
GDN_CONV_DIM = 2 * GDN_QK + GDN_VW
GDN_IN = GDN_CONV_DIM + GDN_VW + 2 * GDN_HEADS
GDN_CHUNK = 64
SSD_EXPAND = 2
SSD_DI = SSD_EXPAND * D_MODEL
SSD_HEADDIM = 64
SSD_HEADS = SSD_DI // SSD_HEADDIM
SSD_GROUPS = 4
SSD_DSTATE = 128
SSD_CONV_DIM = SSD_DI + 2 * SSD_GROUPS * SSD_DSTATE
SSD_IN = SSD_DI + SSD_CONV_DIM + SSD_HEADS
SSD_CHUNK = 64
D_FF = ((8 * D_MODEL + 3 * 256 - 1) // (3 * 256)) * 256

kernel_name = 'hybrid_gdn_ssd_decoder_step'

F32 = jnp.float32


def rms_norm(x, w, eps=NORM_EPS):
    xf = x.astype(F32)
    return xf * lax.rsqrt(jnp.mean(xf * xf, axis=-1, keepdims=True) + eps) * w.astype(F32)


def l2_normalize(x, eps=1e-6):
    return x * lax.rsqrt(jnp.sum(x * x, axis=-1, keepdims=True) + eps)


def causal_conv(x, buf, w, b=None):
    L = x.shape[1]
    xx = jnp.concatenate([buf, x], axis=1)
    out = sum(xx[:, j:j + L] * w[j] for j in range(CONV_W))
    if b is not None:
        out = out + b
    return out, xx[:, L:]


def gated_delta_chunked(q, k, v, g, beta, S0):
    Bsz, L, H, dk = q.shape
    dv = v.shape[-1]
    C = math.gcd(L, GDN_CHUNK)
    N = L // C

    def chunkify(t):
        return jnp.swapaxes(t.reshape((Bsz, N, C, H) + t.shape[3:]), 2, 3)

    q, k, v, g, beta = (chunkify(t) for t in (q, k, v, g, beta))
    G = jnp.cumsum(g, axis=-1)
    idx = jnp.arange(C)
    causal = idx[:, None] >= idx[None, :]
    strict = idx[:, None] > idx[None, :]
    seg = G[..., :, None] - G[..., None, :]
    decay = jnp.where(causal, jnp.exp(jnp.where(causal, seg, 0.0)), 0.0)
    kb = k * beta[..., None]
    M = jnp.where(strict, jnp.einsum('bnhik,bnhjk->bnhij', kb, k) * decay, 0.0)
    A = M + jnp.eye(C, dtype=M.dtype)
    rhs = jnp.concatenate([v * beta[..., None], kb * jnp.exp(G)[..., None]], axis=-1)
    sol = lax.linalg.triangular_solve(A, rhs, left_side=True, lower=True)
    u, w = sol[..., :dv], sol[..., dv:]
    attn = jnp.where(causal, jnp.einsum('bnhik,bnhjk->bnhij', q, k) * decay, 0.0)

    def step(S, inp):
        q_c, k_c, u_c, w_c, attn_c, G_c = inp
        v_new = u_c - jnp.einsum('bhck,bhkv->bhcv', w_c, S)
        o = (jnp.einsum('bhck,bhkv->bhcv', q_c * jnp.exp(G_c)[..., None], S)
             + jnp.einsum('bhij,bhjv->bhiv', attn_c, v_new))
        last = G_c[..., -1]
        S = (S * jnp.exp(last)[..., None, None]
             + jnp.einsum('bhck,bhcv->bhkv', k_c * jnp.exp(last[..., None] - G_c)[..., None], v_new))
        return S, o

    xs = tuple(jnp.moveaxis(t, 1, 0) for t in (q, k, u, w, attn, G))
    S, o = lax.scan(step, S0, xs)
    o = jnp.transpose(o, (1, 0, 3, 2, 4)).reshape(Bsz, L, H, dv)
    return o, S


def ssd_chunked(x, dt, A, Bm, Cm, h0):
    Bsz, L, H, P = x.shape
    G, S = Bm.shape[2], Bm.shape[3]
    Hg = H // G
    C = math.gcd(L, SSD_CHUNK)
    N = L // C
    xdt = (x * dt[..., None]).reshape(Bsz, N, C, G, Hg, P)
    Bc = Bm.reshape(Bsz, N, C, G, S)
    Cc = Cm.reshape(Bsz, N, C, G, S)
    a = jnp.moveaxis((dt * A).reshape(Bsz, N, C, G, Hg), 2, -1)
    Acum = jnp.cumsum(a, axis=-1)
    idx = jnp.arange(C)
    causal = idx[:, None] >= idx[None, :]
    seg = Acum[..., :, None] - Acum[..., None, :]
    decay = jnp.where(causal, jnp.exp(jnp.where(causal, seg, 0.0)), 0.0)
    CB = jnp.einsum('bnigs,bnjgs->bngij', Cc, Bc)
    y_diag = jnp.einsum('bnghij,bnjghp->bnighp', CB[:, :, :, None] * decay, xdt)

    def step(hprev, inp):
        xdt_c, B_c, C_c, A_c = inp
        y_off = (jnp.einsum('bigs,bghps->bighp', C_c, hprev)
                 * jnp.moveaxis(jnp.exp(A_c), -1, 1)[..., None])
        last = A_c[..., -1]
        wts = jnp.exp(last[..., None] - A_c)
        hnew = (hprev * jnp.exp(last)[..., None, None]
                + jnp.einsum('bghj,bjgs,bjghp->bghps', wts, B_c, xdt_c))
        return hnew, y_off

    xs = tuple(jnp.moveaxis(t, 1, 0) for t in (xdt, Bc, Cc, Acum))
    hN, y_off = lax.scan(step, h0.reshape(Bsz, G, Hg, P, S), xs)
    y = y_diag + jnp.moveaxis(y_off, 0, 1)
    return y.reshape(Bsz, L, H, P), hN.reshape(Bsz, H, P, S)


def gdn_mixer(h, S0, conv0, w_in, conv_w, A_log, dt_bias, norm_w, w_out):
    Bsz, L, _ = h.shape
    proj = (h @ w_in).astype(F32)
    qkv, z, b, a = jnp.split(proj, [GDN_CONV_DIM, GDN_CONV_DIM + GDN_VW,
                                    GDN_CONV_DIM + GDN_VW + GDN_HEADS], axis=-1)
    qkv, conv_new = causal_conv(qkv, conv0.astype(F32), conv_w.astype(F32))
    qkv = jax.nn.silu(qkv)
    q, k, v = jnp.split(qkv, [GDN_QK, 2 * GDN_QK], axis=-1)
    q = l2_normalize(q.reshape(Bsz, L, GDN_HEADS, GDN_DK)) * (GDN_DK ** -0.5)
    k = l2_normalize(k.reshape(Bsz, L, GDN_HEADS, GDN_DK))
    v = v.reshape(Bsz, L, GDN_HEADS, GDN_DV)
    beta = jax.nn.sigmoid(b)
    g = -jnp.exp(A_log.astype(F32)) * jax.nn.softplus(a + dt_bias.astype(F32))
    o, S = gated_delta_chunked(q, k, v, g, beta, S0.astype(F32))
    o = rms_norm(o, norm_w) * jax.nn.silu(z.reshape(Bsz, L, GDN_HEADS, GDN_DV))
    out = o.reshape(Bsz, L, GDN_VW) @ w_out
    return out, S, conv_new


def ssd_mixer(h, h0, conv0, w_in, conv_w, conv_b, dt_bias, A_log, D_skip, norm_w, w_out):
    Bsz, L, _ = h.shape
    proj = (h @ w_in).astype(F32)
    z, xBC, dt = jnp.split(proj, [SSD_DI, SSD_DI + SSD_CONV_DIM], axis=-1)
    xBC, conv_new = causal_conv(xBC, conv0.astype(F32), conv_w.astype(F32), conv_b.astype(F32))
    xBC = jax.nn.silu(xBC)
    xs, Bm, Cm = jnp.split(xBC, [SSD_DI, SSD_DI + SSD_GROUPS * SSD_DSTATE], axis=-1)
    xs = xs.reshape(Bsz, L, SSD_HEADS, SSD_HEADDIM)
    Bm = Bm.reshape(Bsz, L, SSD_GROUPS, SSD_DSTATE)
    Cm = Cm.reshape(Bsz, L, SSD_GROUPS, SSD_DSTATE)
    dt = jax.nn.softplus(dt + dt_bias.astype(F32))
    A = -jnp.exp(A_log.astype(F32))
    y, hN = ssd_chunked(xs, dt, A, Bm, Cm, h0.astype(F32))
    y = (y + D_skip.astype(F32)[:, None] * xs).reshape(Bsz, L, SSD_DI)
    y = rms_norm(y * jax.nn.silu(z), norm_w)
    return y @ w_out, hN, conv_new


def run_trunk(x, gdn_S, gdn_conv, ssd_h, ssd_conv, params):
    (mix_pre_norm, mix_post_norm, ffn_pre_norm, ffn_post_norm,
     gdn_w_in, gdn_conv_w, gdn_A_log, gdn_dt_bias, gdn_norm_w, gdn_w_out,
     ssd_w_in, ssd_conv_w, ssd_conv_b, ssd_dt_bias, ssd_A_log, ssd_D, ssd_norm_w, ssd_w_out,
     ffn_w_gate, ffn_w_up, ffn_w_down) = params
    new_gS, new_gC, new_sH, new_sC = [], [], [], []
    for i in range(DEPTH):
        j = i // N_MIXERS
        hn = rms_norm(x, mix_pre_norm[i])
        if i % N_MIXERS == 0:
            out, S, cbuf = gdn_mixer(hn, gdn_S[j], gdn_conv[j], gdn_w_in[j], gdn_conv_w[j],
                                     gdn_A_log[j], gdn_dt_bias[j], gdn_norm_w[j], gdn_w_out[j])
            new_gS.append(S.astype(gdn_S.dtype))
            new_gC.append(cbuf.astype(gdn_conv.dtype))
        else:
            out, hs, cbuf = ssd_mixer(hn, ssd_h[j], ssd_conv[j], ssd_w_in[j], ssd_conv_w[j],
                                      ssd_conv_b[j], ssd_dt_bias[j], ssd_A_log[j], ssd_D[j],
                                      ssd_norm_w[j], ssd_w_out[j])
            new_sH.append(hs.astype(ssd_h.dtype))
            new_sC.append(cbuf.astype(ssd_conv.dtype))
        x = x + rms_norm(out, mix_post_norm[i]).astype(x.dtype)
        hn = rms_norm(x, ffn_pre_norm[i])
        f = (jax.nn.silu(hn @ ffn_w_gate[i]) * (hn @ ffn_w_up[i])) @ ffn_w_down[i]
        x = x + rms_norm(f, ffn_post_norm[i]).astype(x.dtype)
    return x, jnp.stack(new_gS), jnp.stack(new_gC), jnp.stack(new_sH), jnp.stack(new_sC)


def setup_inputs(seed: int = 0) -> dict:
    key = jax.random.key(seed)
    ks = iter(jax.random.split(key, 40))

    def nrm(shape, scale):
        return jax.random.normal(next(ks), shape, F32) * scale

    def dt_bias_init(shape):
        dt = jnp.exp(jax.random.uniform(next(ks), shape, F32, math.log(1e-3), math.log(0.1)))
        return dt + jnp.log(-jnp.expm1(-dt))

    def a_log_init(shape):
        return jnp.log(jax.random.uniform(next(ks), shape, F32, 1.0, 16.0))

    def gain(shape):
        return 1.0 + nrm(shape, 0.02)

    return {
        'x_prompt': nrm((BATCH, SEQ, D_MODEL), 1.0),
        'x_sample': nrm((DEC_BATCH, DEC_SEQ, D_MODEL), 1.0),
        'state_gdn_S': nrm((N_GDN, DEC_BATCH, GDN_HEADS, GDN_DK, GDN_DV), 0.05),
        'state_gdn_conv': nrm((N_GDN, DEC_BATCH, CONV_W - 1, GDN_CONV_DIM), 1.0),
        'state_ssd_h': nrm((N_SSD, DEC_BATCH, SSD_HEADS, SSD_HEADDIM, SSD_DSTATE), 0.1),
        'state_ssd_conv': nrm((N_SSD, DEC_BATCH, CONV_W - 1, SSD_CONV_DIM), 1.0),
        'mix_pre_norm': gain((DEPTH, D_MODEL)),
        'mix_post_norm': gain((DEPTH, D_MODEL)),
        'ffn_pre_norm': gain((DEPTH, D_MODEL)),
        'ffn_post_norm': gain((DEPTH, D_MODEL)),
        'gdn_w_in': nrm((N_GDN, D_MODEL, GDN_IN), D_MODEL ** -0.5),
        'gdn_conv_w': nrm((N_GDN, CONV_W, GDN_CONV_DIM), CONV_W ** -0.5),
        'gdn_A_log': a_log_init((N_GDN, GDN_HEADS)),
        'gdn_dt_bias': dt_bias_init((N_GDN, GDN_HEADS)),
        'gdn_norm_w': gain((N_GDN, GDN_DV)),
        'gdn_w_out': nrm((N_GDN, GDN_VW, D_MODEL), GDN_VW ** -0.5),
        'ssd_w_in': nrm((N_SSD, D_MODEL, SSD_IN), D_MODEL ** -0.5),
        'ssd_conv_w': nrm((N_SSD, CONV_W, SSD_CONV_DIM), CONV_W ** -0.5),
        'ssd_conv_b': nrm((N_SSD, SSD_CONV_DIM), 0.02),
        'ssd_dt_bias': dt_bias_init((N_SSD, SSD_HEADS)),
        'ssd_A_log': a_log_init((N_SSD, SSD_HEADS)),
        'ssd_D': 1.0 + nrm((N_SSD, SSD_HEADS), 0.1),
        'ssd_norm_w': gain((N_SSD, SSD_DI)),
        'ssd_w_out': nrm((N_SSD, SSD_DI, D_MODEL), SSD_DI ** -0.5),
        'ffn_w_gate': nrm((DEPTH, D_MODEL, D_FF), D_MODEL ** -0.5),
        'ffn_w_up': nrm((DEPTH, D_MODEL, D_FF), D_MODEL ** -0.5),
        'ffn_w_down': nrm((DEPTH, D_FF, D_MODEL), D_FF ** -0.5),
    }


def reference(x_prompt, x_sample, state_gdn_S, state_gdn_conv, state_ssd_h, state_ssd_conv,
              mix_pre_norm, mix_post_norm, ffn_pre_norm, ffn_post_norm,
              gdn_w_in, gdn_conv_w, gdn_A_log, gdn_dt_bias, gdn_norm_w, gdn_w_out,
              ssd_w_in, ssd_conv_w, ssd_conv_b, ssd_dt_bias, ssd_A_log, ssd_D, ssd_norm_w, ssd_w_out,
              ffn_w_gate, ffn_w_up, ffn_w_down):
    params = (mix_pre_norm, mix_post_norm, ffn_pre_norm, ffn_post_norm,
              gdn_w_in, gdn_conv_w, gdn_A_log, gdn_dt_bias, gdn_norm_w, gdn_w_out,
              ssd_w_in, ssd_conv_w, ssd_conv_b, ssd_dt_bias, ssd_A_log, ssd_D, ssd_norm_w, ssd_w_out,
              ffn_w_gate, ffn_w_up, ffn_w_down)
    nb = x_prompt.shape[0]
    z_gS = jnp.zeros((N_GDN, nb) + state_gdn_S.shape[2:], state_gdn_S.dtype)
    z_gC = jnp.zeros((N_GDN, nb) + state_gdn_conv.shape[2:], state_gdn_conv.dtype)
    z_sH = jnp.zeros((N_SSD, nb) + state_ssd_h.shape[2:], state_ssd_h.dtype)
    z_sC = jnp.zeros((N_SSD, nb) + state_ssd_conv.shape[2:], state_ssd_conv.dtype)
    y_prompt, p_gS, p_gC, p_sH, p_sC = run_trunk(x_prompt, z_gS, z_gC, z_sH, z_sC, params)
    y_sample, s_gS, s_gC, s_sH, s_sC = run_trunk(x_sample, state_gdn_S, state_gdn_conv,
                                                 state_ssd_h, state_ssd_conv, params)
    return (y_prompt, y_sample, p_gS, p_gC, p_sH, p_sC, s_gS, s_gC, s_sH, s_sC)
```

```python
import numpy as np
import sys
import os
KMAX = int(os.environ.get('KMAXOPS', '0'))
import concourse.bass as bass
import concourse.mybir as mybir
from concourse.bass_utils import run_bass_kernel_spmd

F32 = mybir.dt.float32
F32R = mybir.dt.float32r
BF16 = mybir.dt.bfloat16
AF = mybir.ActivationFunctionType
ALU = mybir.AluOpType
AX = mybir.AxisListType


class V:
    __slots__ = ("ap", "keys")

    def __init__(self, ap, keys):
        self.ap = ap
        self.keys = list(keys)

    def __getitem__(self, idx):
        return V(self.ap[idx], self.keys)

    def k(self, *keys):
        return V(self.ap, keys)

    def re(self, s, **kw):
        return V(self.ap.rearrange(s, **kw), self.keys)

    def bc(self, shape):
        return V(self.ap.to_broadcast(list(shape)), self.keys)

    def bitcast(self, dt):
        return V(self.ap.bitcast(dt), self.keys)


class Prog:
    ENG = ("pe", "act", "dve", "pool", "sp")

    def __init__(self, nc, n_dma_sems=48, same_engine_sync=True):
        self.nc = nc
        self.eng = {"pe": nc.tensor, "act": nc.scalar, "dve": nc.vector,
                    "pool": nc.gpsimd, "sp": nc.sync}
        self.ops = {e: [] for e in self.ENG}
        self.cnt = {e: 0 for e in self.ENG}
        self.sem = {e: nc.alloc_semaphore("c_" + e) for e in self.ENG}
        self.dsem = [nc.alloc_semaphore(f"d{i}") for i in range(n_dma_sems)]
        self.dval = [0] * n_dma_sems
        self.dnext = 0
        self.last_w = {}
        self.readers = {}
        self.seen = {e: {} for e in self.ENG}
        self.same = same_engine_sync
        self.n_ops = 0
        self.out_events = []
        self.excl = set()

    def _need(self, e, ev, is_dma_issue=False):
        if ev is None:
            return None
        sem, val, src = ev
        if src == e and not self.same and src != "dma":
            return None
        if src == e and e == "pe":
            return None
        sid = id(sem)
        if self.seen[e].get(sid, 0) >= val:
            return None
        self.seen[e][sid] = val
        return (sem, val)

    def _collect(self, e, reads, writes):
        waits = []
        for k in reads:
            w = self._need(e, self.last_w.get(k))
            if w:
                waits.append(w)
        for k in writes:
            w = self._need(e, self.last_w.get(k))
            if w:
                waits.append(w)
            for ev in self.readers.get(k, {}).values():
                w = self._need(e, ev)
                if w:
                    waits.append(w)
        return waits

    def _commit(self, ev, reads, writes):
        for k in reads:
            d = self.readers.setdefault(k, {})
            old = d.get(id(ev[0]))
            if old is None or old[1] < ev[1]:
                d[id(ev[0])] = ev
        for k in writes:
            self.last_w[k] = ev
            self.readers[k] = {}

    def mark(self, name):
        if os.environ.get('KMARK'):
            print("MARK", name, self.n_ops, file=sys.stderr)

    def _where(self):
        f = sys._getframe(2)
        out = []
        while f is not None and len(out) < 4:
            out.append(f.f_lineno)
            f = f.f_back
        return out

    def op(self, e, fn, reads=(), writes=()):
        if KMAX and self.n_ops >= KMAX:
            return None
        reads = [k for v in reads for k in (v.keys if isinstance(v, V) else [v])]
        writes = [k for v in writes for k in (v.keys if isinstance(v, V) else [v])]
        writes = writes + [k for k in reads if k in self.excl and k not in writes]
        waits = self._collect(e, reads, writes)
        self.cnt[e] += 1
        ev = (self.sem[e], self.cnt[e], e)
        self.ops[e].append((fn, waits, (self.sem[e], 1), self._where()))
        self._commit(ev, reads, writes)
        self.n_ops += 1
        return ev

    def dma(self, q, out, in_, is_output=False, **kw):
        if KMAX and self.n_ops >= KMAX:
            return None
        reads = list(in_.keys)
        writes = list(out.keys)
        waits = self._collect(q, reads, writes)
        i = self.dnext
        self.dnext = (self.dnext + 1) % len(self.dsem)
        sem = self.dsem[i]
        if self.dval[i] > 0:
            w = self._need(q, (sem, self.dval[i], "dma"))
            if w:
                waits.append(w)
        self.dval[i] += 16
        ev = (sem, self.dval[i], "dma")
        oap, iap = out.ap, in_.ap
        self.ops[q].append((lambda eng: eng.dma_start(out=oap, in_=iap, **kw), waits, (sem, 16), self._where()))
        self._commit(ev, reads, writes)
        if is_output:
            self.out_events.append(ev)
        self.n_ops += 1
        return ev

    def finish(self):
        fin = []
        for ev in self.out_events:
            w = self._need("sp", ev)
            if w:
                fin.append(w)
        for e in self.ENG:
            if e != "sp" and self.cnt[e] > 0:
                w = self._need("sp", (self.sem[e], self.cnt[e], e))
                if w:
                    fin.append(w)
        nc = self.nc
        ops = self.ops

        def replay(e, eng):
            for fn, waits, inc, where in ops[e]:
                for (s, v) in waits:
                    eng.wait_ge(s, v)
                if fn is None:
                    continue
                try:
                    ins = fn(eng)
                except Exception:
                    print("FAILED OP emitted at lines", where, file=sys.stderr)
                    raise
                ins.then_inc(inc[0], inc[1])
            if e == "sp":
                for (s, v) in fin:
                    eng.wait_ge(s, v)

        with nc.Block() as block:
            @block.sync
            def _(eng):
                replay("sp", eng)

            @block.tensor
            def _(eng):
                replay("pe", eng)

            @block.scalar
            def _(eng):
                replay("act", eng)

            @block.vector
            def _(eng):
                replay("dve", eng)

            @block.gpsimd
            def _(eng):
                replay("pool", eng)

    def mm(self, out, lhsT, rhs, start=True, stop=True, **kw):
        o, l, r = out.ap, lhsT.ap, rhs.ap
        return self.op("pe", lambda eng: eng.matmul(o, l, r, start=start, stop=stop, **kw),
                       reads=[lhsT, rhs], writes=[out])

    def tr(self, out, in_, ident):
        o, i, d = out.ap, in_.ap, ident.ap
        return self.op("pe", lambda eng: eng.transpose(o, i, d), reads=[in_, ident], writes=[out])

    def act(self, out, in_, func, bias=None, scale=1.0, e="act", **kw):
        o, i = out.ap, in_.ap
        reads = [in_]
        b = bias
        if isinstance(bias, V):
            reads.append(bias)
            b = bias.ap
        s = scale
        if isinstance(scale, V):
            reads.append(scale)
            s = scale.ap
        kw2 = dict(kw)
        if b is not None:
            kw2["bias"] = b
        return self.op("act", lambda eng: eng.activation(out=o, in_=i, func=func, scale=s, **kw2),
                       reads=reads, writes=[out])

    def tt(self, e, out, in0, in1, op):
        o, a, b = out.ap, in0.ap, in1.ap
        return self.op(e, lambda eng: eng.tensor_tensor(out=o, in0=a, in1=b, op=op),
                       reads=[in0, in1], writes=[out])

    def ts(self, e, out, in0, s1, op0, s2=None, op1=None):
        o, a = out.ap, in0.ap
        reads = [in0]
        x1 = s1
        if isinstance(s1, V):
            reads.append(s1)
            x1 = s1.ap
        x2 = s2
        if isinstance(s2, V):
            reads.append(s2)
            x2 = s2.ap
        if op1 is None:
            return self.op(e, lambda eng: eng.tensor_scalar(out=o, in0=a, scalar1=x1, scalar2=None, op0=op0),
                           reads=reads, writes=[out])
        return self.op(e, lambda eng: eng.tensor_scalar(out=o, in0=a, scalar1=x1, scalar2=x2, op0=op0, op1=op1),
                       reads=reads, writes=[out])

    def stt(self, out, in0, s, in1, op0, op1):
        o, a, b = out.ap, in0.ap, in1.ap
        reads = [in0, in1]
        x = s
        if isinstance(s, V):
            reads.append(s)
            x = s.ap
        return self.op("dve", lambda eng: eng.scalar_tensor_tensor(out=o, in0=a, scalar=x, in1=b, op0=op0, op1=op1),
                       reads=reads, writes=[out])

    def copy(self, e, out, in_):
        o, i = out.ap, in_.ap
        if e == "act":
            return self.op("act", lambda eng: eng.copy(out=o, in_=i), reads=[in_], writes=[out])
        return self.op(e, lambda eng: eng.tensor_copy(out=o, in_=i), reads=[in_], writes=[out])

    def memset(self, e, out, val):
        o = out.ap
        return self.op(e, lambda eng: eng.memset(o, val), writes=[out])

    def recip(self, e, out, in_):
        o, i = out.ap, in_.ap
        return self.op(e, lambda eng: eng.reciprocal(out=o, in_=i), reads=[in_], writes=[out])

    def barrier(self):
        evs = [(self.sem[e], self.cnt[e], e) for e in self.ENG if self.cnt[e] > 0]
        evs += [(self.dsem[i], self.dval[i], "dma") for i in range(len(self.dsem)) if self.dval[i] > 0]
        for e in self.ENG:
            waits = []
            for ev in evs:
                if ev[2] == e and e != "dma":
                    pass
                w = self._need(e, ev)
                if w:
                    waits.append(w)
            if waits:
                self.ops[e].append((None, waits, None, None))

import math
from contextlib import ExitStack

MULT, ADD, SUB, MAX = ALU.mult, ALU.add, ALU.subtract, ALU.max


class Blk:
    pass


def make_blocks(L, chunks_per_block):
    blocks = []
    c = 0
    for bi, nchb in enumerate(chunks_per_block):
        b = Blk()
        b.idx = bi
        b.c0 = c * 128
        b.first = bi == 0
        b.last = bi == len(chunks_per_block) - 1
        b.chunks = [(i * 128, 128, 0) for i in range(nchb)]
        npr = nchb * 128
        b.tiles = []
        o = 0
        while o < npr:
            n = min(512, npr - o)
            b.tiles.append((o, n, False))
            o += n
        b.n = npr
        if b.last:
            b.chunks.append((npr, 64, 1))
            b.tiles.append((npr, 64, True))
            b.n = npr + 64
        c += nchb
        blocks.append(b)
    assert c * 128 == L
    return blocks


def build(L=2048, depth=4, chunks_per_block=None, inv_dt="f32", ew_eng="dve"):
    nch = L // 128
    if chunks_per_block is None:
        chunks_per_block = [4] * (nch // 4) if nch % 4 == 0 else [nch]
    blocks = make_blocks(L, chunks_per_block)
    NB = max(b.n for b in blocks)
    T = L + 64
    nc = bass.Bass("TRN2", target_bir_lowering=False, dynamic_dma_scratch_size=4096)
    P = Prog(nc)

    def din(name, shape):
        return V(nc.dram_tensor(name, list(shape), F32, kind="ExternalInput").ap(), ["D_" + name])

    def dout(name, shape):
        return V(nc.dram_tensor(name, list(shape), F32, kind="ExternalOutput").ap(), ["D_" + name])

    xp = din("xp", [L, 1024]); xs = din("xs", [64, 1024])
    gS = din("gS", [2, 16, 8, 128, 128]); gC = din("gC", [2, 48, 3072])
    sH = din("sH", [2, 16, 32, 64, 128]); sC = din("sC", [2, 48, 3072])
    nw_d = din("nw", [128, 4 * 4 * 8])
    gWt = din("gWt", [2, 32, 128, 1024]); gWba = din("gWba", [2, 128, 128])
    gcw_d = din("gcw", [128, 2 * 24 * 4]); gAd_d = din("gAd", [128, 32]); gnw_d = din("gnw", [128, 2])
    gWo = din("gWo", [2, 8, 128, 1024])
    sWt = din("sWt", [2, 40, 128, 1024]); sWdt = din("sWdt", [2, 128, 256])
    scw_d = din("scw", [128, 2 * 24 * 4]); scb_d = din("scb", [128, 48]); sAd_d = din("sAd", [128, 128])
    sD_d = din("sD", [128, 32]); snw_d = din("snw", [128, 32]); sWo = din("sWo", [2, 8, 128, 2048])
    fWg = din("fWg", [4, 22, 128, 1024]); fWu = din("fWu", [4, 22, 128, 1024]); fWd = din("fWd", [4, 8, 128, 2816])
    ident_d = din("ident", [128, 128]); masks_d = din("masks", [128, 1024])
    cm_d = din("cm", [128, 1024]); bm_d = din("bm", [64, 16])

    yp = dout("yp", [L, 1024]); ys = dout("ys", [64, 1024])
    o_gS = dout("o_gS", [2, 8, 128, 128]); o_gC = dout("o_gC", [2, 3, 3072])
    o_sH = dout("o_sH", [2, 32, 64, 128]); o_sC = dout("o_sC", [2, 3, 3072])
    o_sgS = dout("o_sgS", [2, 16, 8, 128, 128]); o_sgC = dout("o_sgC", [2, 48, 3072])
    o_ssH = dout("o_ssH", [2, 16, 32, 64, 128]); o_ssC = dout("o_ssC", [2, 48, 3072])

    uid = [0]

    def sb(name, shape, dt=F32):
        uid[0] += 1
        nm = f"{name}_{uid[0]}"
        return V(nc.alloc_sbuf_tensor(nm, list(shape), dt).ap(), [nm])

    class Ring:
        def __init__(self, name, shape, dt, n, alloc=None):
            self.slots = [(alloc or sb)(f"{name}{i}", shape, dt) for i in range(n)]
            self.i = 0

        def next(self):
            s = self.slots[self.i % len(self.slots)]
            self.i += 1
            return s

    class Scope:
        def __enter__(self):
            self.st = ExitStack()
            return self

        def sb(self, name, shape, dt=F32):
            uid[0] += 1
            nm = f"{name}_{uid[0]}"
            t = self.st.enter_context(nc.sbuf_tensor(nm, list(shape), dt))
            return V(t.ap(), [nm])

        def __exit__(self, *a):
            P.barrier()
            self.st.close()
            return False

    x = sb("x", [128, 8, T])
    hn = sb("hn", [128, 8, NB], BF16)
    ob = sb("ob", [128, 8, NB])
    ident = sb("ident", [128, 128]); identb = sb("identb", [128, 128], BF16)
    ones_f = sb("ones_f", [128, 128]); onesb = sb("onesb", [128, 128], BF16)
    masks = sb("masks", [128, 2, 4, 128])
    cm = sb("cm", [128, 16, 64], BF16); bm = sb("bm", [64, 16])
    nw = sb("nw", [128, 4, 4, 8])
    gcw = sb("gcw", [128, 2, 24, 4]); gAd = sb("gAd", [128, 2, 16]); gnw = sb("gnw", [128, 2])
    scw = sb("scw", [128, 2, 24, 4]); scb = sb("scb", [128, 2, 24]); sAd = sb("sAd", [128, 2, 64])
    sD = sb("sD", [128, 2, 16]); snw = sb("snw", [128, 2, 16])
    epsc = sb("epsc", [128, 1]); onec = sb("onec", [128, 1]); lnq = sb("lnq", [128, 1]); zeroc = sb("zeroc", [128, 1])
    hist = sb("hist", [128, 24, 3]); histb = sb("histb", [128, 24, 3], BF16)
    stateA = sb("stateA", [128, 2048])
    stateB = sb("stateB", [128, 2048], BF16)
    WR = Ring("wr", [128, 8, 128], BF16, 5)
    PS = Ring("ps", [128, 512], F32, 4, alloc=lambda n, s, d: V(nc.alloc_psum_tensor(n, list(s), d).ap(), [n]))
    PD = [V(nc.alloc_psum_tensor(f"pd{i}", [128, 512], F32).ap(), [f"pd{i}"]) for i in range(4)]
    PY = PD[0]
    P.excl = set([f"pd{i}" for i in range(4)] + [k for s_ in PS.slots for k in s_.keys])
    SQ = Ring("sq", [128, 512], BF16, 3)
    RS = Ring("rs", [128, 512], F32, 2)

    def xt(c0, n):
        return V(x.ap[:, :, c0:c0 + n], [("x", c0 // 128 + i) for i in range((n + 127) // 128)])

    def xt1(dt, c0, n):
        return V(x.ap[:, dt, c0:c0 + n], [("x", c0 // 128 + i) for i in range((n + 127) // 128)])

    P.dma("sp", ident, ident_d)
    P.dma("sp", masks.re("p a b c -> p (a b c)"), masks_d)
    P.dma("pool", cm.re("p a b -> p (a b)"), cm_d)
    P.dma("sp", bm, bm_d)
    P.dma("sp", nw.re("p a b c -> p (a b c)"), nw_d)
    P.dma("sp", gcw.re("p a b c -> p (a b c)"), gcw_d); P.dma("sp", gAd.re("p a b -> p (a b)"), gAd_d)
    P.dma("sp", gnw, gnw_d)
    P.dma("sp", scw.re("p a b c -> p (a b c)"), scw_d); P.dma("sp", scb.re("p a b -> p (a b)"), scb_d)
    P.dma("sp", sAd.re("p a b -> p (a b)"), sAd_d); P.dma("sp", sD.re("p a b -> p (a b)"), sD_d)
    P.dma("sp", snw.re("p a b -> p (a b)"), snw_d)
    P.copy("dve", identb, ident)

    P.memset("dve", ones_f, 1.0); P.memset("dve", onesb, 1.0)

    P.memset("dve", epsc, 1e-6); P.memset("dve", onec, 1.0); P.memset("dve", zeroc, 0.0)
    P.memset("dve", lnq, -0.5 * math.log(128.0))

    def M(case, which, C):
        return masks[:C, case, which, :C]

    with Scope() as sc0:
        XIN = Ring("xin", [128, 1024], F32, 2, alloc=sc0.sb)
        for g in range(T // 128 + (1 if T % 128 else 0)):
            c0 = g * 128
            n = min(128, T - c0)
            t = XIN.next()
            if c0 < L:
                P.dma("sp", t[:n, :], xp[c0:c0 + n, :])
            else:
                P.dma("sp", t[:n, :], xs[c0 - L:c0 - L + n, :])
            for half in range(2):
                pb = PS.next()
                for q in range(4):
                    dt = half * 4 + q
                    P.tr(pb[:, q * 128:q * 128 + n], t[:n, dt * 128:(dt + 1) * 128], ident[:n, :n])
                P.copy("act" if half else "dve", V(x.ap[:, half * 4:half * 4 + 4, c0:c0 + n], [("x", g)]),
                       pb.re("p (q c) -> p q c", q=4)[:, :, :n])

    wq = ["pool"]

    def load_w(dram_v):
        s = WR.next()
        P.dma("pool", s.re("p a b -> p (a b)"), dram_v)
        return s

    def rstd_from_ps(pb, n, scale, rs, extra_bias=None):
        P.act(rs[:, :n], pb[:, :n], AF.Ln, bias=epsc, scale=scale)
        P.act(rs[:, :n], rs[:, :n], AF.Exp, scale=-0.5, bias=(extra_bias if extra_bias is not None else zeroc))

    def prenorm(l, which, blk):
        for (lc0, n, _) in blk.tiles:
            c0 = blk.c0 + lc0
            pb = PS.next()
            for dt in range(8):
                sq = SQ.next()
                P.act(sq[:, :n], xt1(dt, c0, n), AF.Square)
                P.mm(pb[:, :n], onesb, sq[:, :n], start=dt == 0, stop=dt == 7)
            rs = RS.next()
            rstd_from_ps(pb, n, 1.0 / 1024, rs)
            for dt in range(8):
                P.stt(hn[:, dt, lc0:lc0 + n].k(("hn", lc0)), xt1(dt, c0, n), nw[:, l, which, dt:dt + 1],
                      rs[:, :n], MULT, MULT)

    def hn_t(kc, lc0, n, blk):
        for (t0, tn, _) in blk.tiles:
            if t0 <= lc0 < t0 + tn:
                return V(hn.ap[:, kc, lc0:lc0 + n], [("hn", t0)])
        raise AssertionError

    def postnorm_add(l, which, blk):
        for (lc0, n, _) in blk.tiles:
            c0 = blk.c0 + lc0
            pb = PS.next()
            for dt in range(8):
                sq = SQ.next()
                P.act(sq[:, :n], ob[:, dt, lc0:lc0 + n].k(("ob", dt, lc0)), AF.Square)
                P.mm(pb[:, :n], onesb, sq[:, :n], start=dt == 0, stop=dt == 7)
            rs = RS.next()
            rstd_from_ps(pb, n, 1.0 / 1024, rs)
            for dt in range(8):
                o = ob[:, dt, lc0:lc0 + n].k(("ob", dt, lc0))
                P.tt("dve", o, o, rs[:, :n], MULT)
                P.stt(xt1(dt, c0, n), o, nw[:, l, which, dt:dt + 1], xt1(dt, c0, n), MULT, ADD)

    def softplus(out, xin, C, w, tmpa, tmpb):
        P.act(tmpa[:C, :w], xin, AF.Abs, bias=zeroc[:C])
        P.act(tmpa[:C, :w], tmpa[:C, :w], AF.Exp, scale=-1.0, bias=zeroc[:C])
        P.act(tmpa[:C, :w], tmpa[:C, :w], AF.Ln, scale=1.0, bias=onec[:C])
        P.stt(out, xin, 0.0, tmpa[:C, :w], MAX, ADD)

    def conv_tile(pb, n, is_sample, l_conv_w, ft, raw, acc, samp_hist_loader):
        if not is_sample:
            P.copy("dve", raw[:, 0:3], hist[:, ft, :].k(("hist", ft)))
            P.copy("act", raw[:, 3:3 + n], pb[:, :n])
            P.copy("dve", hist[:, ft, :].k(("hist", ft)), raw[:, n:n + 3])
            v = lambda j: raw[:, j:j + n]
            a = acc[:, :n]
        else:
            r7 = raw[:, 0:112].re("p (s t) -> p s t", t=7)
            samp_hist_loader(r7[:, :, 0:3])
            P.copy("act", r7[:, :, 3:7], pb[:, 0:64].re("p (s t) -> p s t", t=4))
            v = lambda j: r7[:, :, j:j + 4]
            a = acc[:, 0:64].re("p (s t) -> p s t", t=4)
        P.ts("dve", a, v(0), l_conv_w[:, 0:1], MULT)
        for j in range(1, 4):
            P.stt(a, v(j), l_conv_w[:, j:j + 1], a, MULT, ADD)

    def build_diag(taps, DG):
        dg = DG.next()
        for jj in range(4):
            P.ts("dve", dg[:, jj, :], ident, taps[:, jj:jj + 1], MULT)
        return dg

    def conv_tile_pe(pb, n, is_sample, dg, ft, rawb, samp_hist_loader):
        pc = PS.next()
        if not is_sample:
            P.copy("dve", rawb[:, 0:3], histb[:, ft, :].k(("histb", ft)))
            P.copy("act", rawb[:, 3:3 + n], pb[:, :n])
            P.copy("dve", histb[:, ft, :].k(("histb", ft)), rawb[:, n:n + 3])
            P.copy("dve", hist[:, ft, :].k(("hist", ft)), pb[:, n - 3:n])
            for jj in range(4):
                P.mm(pc[:, :n], dg[:, jj, :], rawb[:, jj:jj + n], start=jj == 0, stop=jj == 3)
            return pc[:, :n]
        r7 = rawb[:, 0:112].re("p (s t) -> p s t", t=7)
        samp_hist_loader(r7[:, :, 0:3])
        P.copy("act", r7[:, :, 3:7], pb[:, 0:64].re("p (s t) -> p s t", t=4))
        pcv = pc[:, 0:64].re("p (s t) -> p s t", t=4)
        for jj in range(4):
            P.mm(pcv, dg[:, jj, :], r7[:, :, jj:jj + 4], start=jj == 0, stop=jj == 3)
        return pc[:, 0:64]

    def conv_state_out_pe(l_out_p, l_out_s, ft, pbs, stg):
        pb = PS.next()
        P.tr(pb[:3, 0:128], hist[:, ft, :].k(("hist", ft)), ident)
        P.copy("dve", stg[:, 0:48].re("p (s t) -> p s t", t=3), pbs[:, 0:64].re("p (s t) -> p s t", t=4)[:, :, 1:4])
        P.tr(pb[:48, 128:256], stg[:, 0:48], ident)
        P.copy("act", stg[:48, 64:192], pb[:48, 128:256])
        P.copy("dve", stg[:3, 192:320], pb[:3, 0:128])
        P.dma("sp", l_out_p[:, ft * 128:(ft + 1) * 128], stg[:3, 192:320], is_output=True)
        P.dma("sp", l_out_s[:, ft * 128:(ft + 1) * 128], stg[:48, 64:192], is_output=True)

    def conv_state_out(l_out_p, l_out_s, ft, raw, stg):
        pb = PS.next()
        P.tr(pb[:3, 0:128], hist[:, ft, :].k(("hist", ft)), ident)
        r7 = raw[:, 0:112].re("p (s t) -> p s t", t=7)
        P.copy("dve", stg[:, 0:48].re("p (s t) -> p s t", t=3), r7[:, :, 4:7])
        P.tr(pb[:48, 128:256], stg[:, 0:48], ident)
        P.copy("act", stg[:48, 64:192], pb[:48, 128:256])
        P.copy("dve", stg[:3, 192:320], pb[:3, 0:128])
        P.dma("sp", l_out_p[:, ft * 128:(ft + 1) * 128], stg[:3, 192:320], is_output=True)
        P.dma("sp", l_out_s[:, ft * 128:(ft + 1) * 128], stg[:48, 64:192], is_output=True)

    def samp_hist(l_in, ft, HS):
        def f(dst):
            t = HS.next()
            P.dma("sp", t[:48, :], l_in[:, ft * 128:(ft + 1) * 128])
            pb = PS.next()
            P.tr(pb[:, 0:48], t[:48, :], ident[:48, :48])
            P.copy("dve", dst, pb[:, 0:48].re("p (s t) -> p s t", t=3))
        return f

    def gdn_block(l, j, blk):
        with Scope() as sc:
            S = stateA[:, 0:1024].re("p (h v) -> p h v", h=8)
            Sb = stateB[:, 0:1024].re("p (h v) -> p h v", h=8)
            if blk.first:
                P.memset("dve", stateA, 0.0); P.memset("dve", stateB, 0.0)
                P.memset("dve", hist, 0.0); P.memset("dve", histb, 0.0)
            prenorm(l, 0, blk)
            ncb = len(blk.chunks)
            wba = sc.sb("wba", [128, 8, 16], BF16)
            P.dma("pool", wba.re("p a b -> p (a b)"), gWba[j])
            beta = sc.sb("beta", [128, ncb, 8]); nbeta = sc.sb("nbeta", [128, ncb, 8])
            gg = sc.sb("gg", [128, ncb, 8]); wdec = sc.sb("wdec", [128, ncb, 8])
            negA = sc.sb("negA", [128, 8])
            ta = sc.sb("ta", [128, 16]); tb = sc.sb("tb", [128, 16]); tcn = sc.sb("tcn", [128, 16])
            P.act(negA, gAd[:, j, 0:8], AF.Exp, bias=zeroc)
            P.ts("dve", negA, negA, -1.0, MULT)
            for ci, (lc0, C, case) in enumerate(blk.chunks):
                pb = PS.next()
                for kc in range(8):
                    P.mm(pb[:C, 0:16], hn_t(kc, lc0, C, blk), wba[:, kc, :], start=kc == 0, stop=kc == 7)
                P.act(ta[:C, 0:8], pb[:C, 0:8], AF.Exp, scale=-1.0, bias=zeroc[:C])
                P.ts("dve", ta[:C, 0:8], ta[:C, 0:8], 1.0, ADD)
                P.recip("dve", beta[:C, ci, :], ta[:C, 0:8])
                P.ts("dve", nbeta[:C, ci, :], beta[:C, ci, :], -1.0, MULT)
                P.tt("dve", tb[:C, 0:8], pb[:C, 8:16], gAd[:C, j, 8:16], ADD)
                softplus(tcn[:C, 0:8], tb[:C, 0:8], C, 8, ta, None)
                P.tt("dve", gg[:C, ci, :], tcn[:C, 0:8], negA[:C, :], MULT)
                pg = PS.next()
                P.mm(pg[:C, 0:8], M(case, 0, C), gg[:C, ci, :])
                P.mm(pg[:C, 8:16], M(case, 3, C), gg[:C, ci, :])
                P.copy("dve", ta[:C, 0:8], pg[:C, 0:8])
                P.tt("dve", tb[:C, 0:8], pg[:C, 8:16], ta[:C, 0:8], SUB)
                P.act(wdec[:C, ci, :], tb[:C, 0:8], AF.Exp, bias=zeroc[:C])

            P.mark("gdn prelude done")
            NSL = 4
            NCS = 2
            QS = [dict(q=sc.sb("qT", [128, NB], BF16), k=sc.sb("kT", [128, NB], BF16), v=sc.sb("vT", [128, NB], BF16),
                       z=sc.sb("zs", [128, NB], BF16), o=sc.sb("oraw", [128, NB]), id=i) for i in range(NSL)]
            oT = sc.sb("oT", [128, 8, NB], BF16)
            RAW = Ring("raw", [128, 520], BF16, 2, alloc=sc.sb)
            DG = Ring("dg", [128, 4, 128], BF16, 2, alloc=sc.sb)
            ACC = Ring("acc", [128, 512], F32, 2, alloc=sc.sb)
            HS = Ring("hs", [48, 128], F32, 2, alloc=sc.sb)
            stg = sc.sb("stg", [128, 320])
            CS = [dict(TF=Ring("tf", [128, 128], F32, 5, alloc=sc.sb), TBF=Ring("tbf", [128, 128], BF16, 8, alloc=sc.sb),
                       TI=Ring("ti", [128, 128], F32, 4, alloc=sc.sb),
                       TI2=Ring("ti2", [128, 2, 128], F32, 2, alloc=sc.sb),
                       TIB=(Ring("tib", [128, 128], BF16, 9, alloc=sc.sb) if inv_dt == "bf16" else None),
                       pB=PD[2 * i], pW=PD[2 * i + 1]) for i in range(NCS)]
            USE_R = inv_dt == "f32r"
            USE_B = inv_dt == "bf16"
            rr = (lambda v: v.bitcast(F32R)) if USE_R else (lambda v: v)
            if blk.last:
                Ss = sc.sb("Ss", [128, 16, 128]); Ssb = sc.sb("Ssb", [128, 16, 128], BF16)
                keTm = sc.sb("keTm", [128, 16, 64], BF16); kdm = sc.sb("kdm", [64, 16, 128], BF16)

            def tkk(buf, name, lc0, C):
                for (t0, tn, _) in blk.tiles:
                    if t0 <= lc0 < t0 + tn:
                        return V(buf.ap[:, lc0:lc0 + C], [(name, t0)])

            def proj_head(h, qs):
                sid = qs["id"]
                for fi, ft in enumerate([h, 8 + h, 16 + h, 24 + h]):
                    w = load_w(gWt[j, ft])
                    if fi != 3:
                        dg = build_diag(gcw[:, j, ft, :], DG)
                    for (lc0, n, is_s) in blk.tiles:
                        pb = PS.next()
                        for kc in range(8):
                            P.mm(pb[:, :n], w[:, kc, :], hn_t(kc, lc0, n, blk), start=kc == 0, stop=kc == 7)
                        if fi == 3:
                            P.act(qs["z"][:, lc0:lc0 + n].k(("zs", sid, lc0)), pb[:, :n], AF.Silu, bias=zeroc)
                            yield
                            continue
                        raw = RAW.next()
                        pc = conv_tile_pe(pb, n, is_s, dg, ft, raw, samp_hist(gC[j], ft, HS))
                        if blk.last and is_s:
                            conv_state_out_pe(o_gC[j], o_sgC[j], ft, pb, stg)
                        if fi == 2:
                            P.act(qs["v"][:, lc0:lc0 + n].k(("vT", sid, lc0)), pc, AF.Silu, bias=zeroc)
                            yield
                            continue
                        acc = ACC.next()
                        P.act(acc[:, :n], pc, AF.Silu, bias=zeroc)
                        sq = SQ.next()
                        P.act(sq[:, :n], acc[:, :n], AF.Square)
                        pb2 = PS.next()
                        P.mm(pb2[:, :n], onesb, sq[:, :n])
                        rs = RS.next()
                        rstd_from_ps(pb2, n, 1.0, rs, extra_bias=(lnq if fi == 0 else None))
                        dst = (qs["q"] if fi == 0 else qs["k"])
                        P.tt("dve", dst[:, lc0:lc0 + n].k((("qT" if fi == 0 else "kT"), sid, lc0)), acc[:, :n], rs[:, :n], MULT)
                        yield

            def chain_head(h, qs, cs):
                sid = qs["id"]
                TF, TBF, TI, pB, pW = cs["TF"], cs["TBF"], cs["TI"], cs["pB"], cs["pW"]
                TIB = cs["TIB"]; TI2 = cs["TI2"]
                qT, kT, vT, zs, oraw = qs["q"], qs["k"], qs["v"], qs["z"], qs["o"]
                for ci, (lc0, C, case) in enumerate(blk.chunks):
                    samp = case == 1
                    def tk(buf, name):
                        for (t0, tn, _) in blk.tiles:
                            if t0 <= lc0 < t0 + tn:
                                return V(buf.ap[:, lc0:lc0 + C], [(name, sid, t0)])
                    qc = tk(qT, "qT"); kc_ = tk(kT, "kT"); vc = tk(vT, "vT")
                    gcol = gg[:C, ci, h:h + 1]
                    gU = TF.next()
                    P.ts("dve", gU[:C, :C], M(case, 0, C), gcol, MULT)
                    if not samp:
                        yield
                    pA = pW
                    P.mm(pA[:, 0:C], ones_f[:C, :], gU[:C, :C])
                    P.mm(pA[:C, 128:128 + C], M(case, 1, C), gU[:C, :C])
                    P.mm(pA[:C, 256:256 + C], kc_, kc_)
                    P.mm(pA[:C, 384:384 + C], kc_, qc)
                    pA2 = pA[:, 256:512]
                    if not samp:
                        yield
                    EGb = TF.next(); decT = TF.next()
                    P.act(EGb[:, :C], pA[:, 0:C], AF.Exp, bias=zeroc)
                    P.act(decT[:C, :C], pA[:C, 128:128 + C], AF.Exp, bias=zeroc[:C])
                    if not samp:
                        yield
                    dms = TF.next(); dmi = TF.next()
                    P.tt("dve", dms[:C, :C], decT[:C, :C], M(case, 2, C), MULT)
                    P.tt("dve", dmi[:C, :C], decT[:C, :C], M(case, 0, C), MULT)
                    P0 = TIB.next() if USE_B else TI.next()
                    P.stt(rr(P0)[:C, :C], pA2[:C, 0:C], nbeta[:C, ci, h:h + 1], dms[:C, :C], MULT, MULT)
                    attnT = TBF.next()
                    P.tt("dve", attnT[:C, :C], pA2[:C, 128:128 + C], dmi[:C, :C], MULT)
                    if not samp:
                        yield
                    nlev = 1 if samp else 6
                    if USE_B:
                        pIb = pW.bitcast(BF16)
                        P.tr(pIb[:C, 0:C], P0[:C, :C], identb[:C, :C])
                        if not samp:
                            yield
                        Pt = TIB.next()
                        P.copy("act", Pt[:C, :C], pIb[:C, 0:C])
                        TtB = TIB.next()
                        P.tt("dve", TtB[:C, :C], P0[:C, :C], identb[:C, :C], ADD)
                        Tt = TI.next()
                        P.tt("dve", Tt[:C, :C], P0[:C, :C], ident[:C, :C], ADD)
                        Pk = P0
                        if not samp:
                            yield
                        for lev in range(1, nlev + 1):
                            pL = pW
                            P.mm(pL[:C, 0:C], Pk[:C, :C], Pt[:C, :C])
                            if lev < nlev:
                                P.mm(pL[:C, 128:128 + C], Pt[:C, :C], Pk[:C, :C])
                            if not samp:
                                yield
                            Ptn = TIB.next()
                            P.copy("act", Ptn[:C, :C], pL[:C, 0:C])
                            if lev < nlev:
                                Pkn = TIB.next()
                                P.copy("dve", Pkn[:C, :C], pL[:C, 128:128 + C])
                            if not samp:
                                yield
                            P.mm(pL[:C, 256:256 + C], Ptn[:C, :C], TtB[:C, :C])
                            if not samp:
                                yield
                            TtBn = TIB.next()
                            P.tt("dve", TtBn[:C, :C], pL[:C, 256:256 + C], Tt[:C, :C], ADD)
                            if lev < nlev:
                                Ttn = TI.next()
                                P.tt("dve", Ttn[:C, :C], pL[:C, 256:256 + C], Tt[:C, :C], ADD)
                                Tt = Ttn
                            TtB = TtBn; Pt = Ptn
                            if lev < nlev:
                                Pk = Pkn
                            if not samp:
                                yield
                    else:
                        pI = pW
                        P.tr(pI[:C, 0:C], P0[:C, :C], ident[:C, :C])
                        if not samp:
                            yield
                        PU = TI2.next()
                        Pt = TI.next()
                        P.copy("act", rr(Pt)[:C, :C], pI[:C, 0:C])
                        P.copy("dve", rr(PU)[:C, 0, :C], P0[:C, :C])
                        P.tt("dve", rr(PU)[:C, 1, :C], P0[:C, :C], ident[:C, :C], ADD)
                        if not samp:
                            yield
                        nr = nlev + 1
                        for r in range(1, nr + 1):
                            pL = pW
                            last = r == nr
                            if r == 1:
                                P.mm(pL[:C, 0:C], rr(Pt)[:C, :C], rr(PU)[:C, 0, :C])
                            elif not last:
                                P.mm(pL.re("p (a c) -> p a c", a=4)[:C, 0:2, :C], rr(Pt)[:C, :C], rr(PU)[:C, :, :C])
                            else:
                                P.mm(pL[:C, 128:128 + C], rr(Pt)[:C, :C], rr(PU)[:C, 1, :C])
                            if not last:
                                P.mm(pL[:C, 256:256 + C], rr(PU)[:C, 0, :C], rr(Pt)[:C, :C])
                            if not samp:
                                yield
                            if last:
                                Tt = TI.next()
                                P.tt("dve", rr(Tt)[:C, :C], pL[:C, 128:128 + C], PU[:C, 1, :C], ADD)
                            else:
                                PUn = TI2.next()
                                Ptn = TI.next()
                                P.copy("act", rr(Ptn)[:C, :C], pL[:C, 256:256 + C])
                                if r < nr - 1:
                                    P.copy("dve", rr(PUn)[:C, 0, :C], pL[:C, 0:C])
                                if r == 1:
                                    P.copy("dve", rr(PUn)[:C, 1, :C], PU[:C, 1, :C])
                                else:
                                    P.tt("dve", rr(PUn)[:C, 1, :C], pL[:C, 128:128 + C], PU[:C, 1, :C], ADD)
                                PU = PUn; Pt = Ptn
                            if not samp:
                                yield
                        TtB = TBF.next()
                        P.copy("act", TtB[:C, :C], Tt[:C, :C])
                    keT = TBF.next(); qeT = TBF.next()
                    P.tt("dve", keT[:, :C], kc_, EGb[:, :C], MULT)
                    P.tt("dve", qeT[:, :C], qc, EGb[:, :C], MULT)
                    pT = pW.bitcast(BF16)
                    P.tr(pT[:C, 0:128], kc_, identb)
                    P.tr(pT[:C, 128:256], vc, identb)
                    if not samp:
                        yield
                    kdec = TBF.next(); vtok = TBF.next()
                    P.act(kdec[:C, :], pT[:C, 0:128], AF.Copy, scale=wdec[:C, ci, h:h + 1])
                    P.copy("dve", vtok[:C, :], pT[:C, 128:256])
                    if not samp:
                        yield
                    R = TBF.next(); vnew = TBF.next()
                    if not samp:
                        Sh = S[:, h, :].k(("S", h)); Shb = Sb[:, h, :].k(("Sb", h))
                        P.mm(pB[:C, 0:128], keT[:, :C], Shb)
                        yield
                        P.tt("dve", R[:C, :], vtok[:C, :], pB[:C, 0:128], SUB)
                        yield
                        P.mm(pB[:C, 128:256], TtB[:C, :C], R[:C, :])
                        yield
                        P.act(vnew[:C, :], pB[:C, 128:256], AF.Copy, scale=beta[:C, ci, h:h + 1])
                        yield
                        P.mm(pB[:, 256:256 + C], Shb, qeT[:, :C], start=True, stop=False)
                        P.mm(pB[:, 256:256 + C], vnew[:C, :], attnT[:C, :C], start=False, stop=True)
                        P.mm(pB[:, 384:512], kdec[:C, :], vnew[:C, :])
                        yield
                        P.stt(Sh, Sh, EGb[:, C - 1:C], pB[:, 384:512], MULT, ADD)
                        P.copy("act", oraw[:, lc0:lc0 + C].k(("oraw", sid, lc0)), pB[:, 256:256 + C])
                        P.copy("act", Shb, Sh)
                        yield
                    else:
                        P.dma("sp", Ss, gS[j, :, h].re("s d v -> d s v"))
                        P.copy("act", Ssb, Ss)
                        P.tt("dve", keTm, keT[:, 0:64].re("p (o c) -> p o c", o=1).bc([128, 16, 64]), cm, MULT)
                        for s_ in range(16):
                            P.mm(pB[:C, 0:128], keTm[:, s_, :], Ssb[:, s_, :], start=s_ == 0, stop=s_ == 15)
                        P.tt("dve", R[:C, :], vtok[:C, :], pB[:C, 0:128], SUB)
                        P.mm(pB[:C, 128:256], TtB[:C, :C], R[:C, :])
                        P.act(vnew[:C, :], pB[:C, 128:256], AF.Copy, scale=beta[:C, ci, h:h + 1])
                        P.mm(pB[:, 256:256 + C], vnew[:C, :], attnT[:C, :C], start=True, stop=False)
                        for s_ in range(16):
                            P.mm(pB[:, 256 + 4 * s_:260 + 4 * s_], Ssb[:, s_, :], qeT[:, 4 * s_:4 * s_ + 4],
                                 start=False, stop=s_ == 15)
                        P.copy("act", oraw[:, lc0:lc0 + C].k(("oraw", sid, lc0)), pB[:, 256:256 + C])
                        P.tt("dve", kdm, kdec[:C, :].re("p (o c) -> p o c", o=1).bc([64, 16, 128]),
                             bm.re("p (s o) -> p s o", o=1).bc([64, 16, 128]), MULT)
                        for s4 in range(4):
                            pS = PS.next()
                            for q in range(4):
                                s_ = s4 * 4 + q
                                P.mm(pS[:, q * 128:(q + 1) * 128], kdm[:, s_, :], vnew[:C, :])
                            for q in range(4):
                                s_ = s4 * 4 + q
                                P.stt(Ss[:, s_, :], Ss[:, s_, :], EGb[:, 4 * s_ + 3:4 * s_ + 4],
                                      pS[:, q * 128:(q + 1) * 128], MULT, ADD)
                        P.dma("sp", o_sgS[j, :, h].re("s d v -> d s v"), Ss, is_output=True)
                        yield
                if blk.last:
                    P.dma("sp", o_gS[j, h], S[:, h, :].k(("S", h)), is_output=True)
                for (lc0, n, _) in blk.tiles:
                    orw = V(oraw.ap[:, lc0:lc0 + n], [("oraw", sid, lc0 + i * 128) for i in range((n + 127) // 128)])
                    sq = SQ.next()
                    P.act(sq[:, :n], orw, AF.Square)
                    pb2 = pW
                    P.mm(pb2[:, :n], onesb, sq[:, :n])
                    rs = RS.next()
                    rstd_from_ps(pb2, n, 1.0 / 128, rs)
                    P.stt(orw, orw, gnw[:, j:j + 1], rs[:, :n], MULT, MULT)
                    P.tt("dve", oT[:, h, lc0:lc0 + n].k(("oT", h, lc0)), orw, zs[:, lc0:lc0 + n].k(("zs", sid, lc0)), MULT)
                    yield

            def proj_pair(hs):
                for hh in hs:
                    yield from proj_head(hh, QS[hh % NSL])

            def run_all(gens):
                gens = list(gens)
                while gens:
                    for g_ in list(gens):
                        try:
                            next(g_)
                        except StopIteration:
                            gens.remove(g_)

            run_all([proj_pair([0, 1])])
            for pr in range(4):
                gens = [chain_head(2 * pr + a, QS[(2 * pr + a) % NSL], CS[a]) for a in range(2)]
                if pr < 3:
                    gens.append(proj_pair([2 * pr + 2, 2 * pr + 3]))
                run_all(gens)
            P.mark("gdn heads done")
            for d in range(8):
                w = load_w(gWo[j, d])
                for (lc0, n, _) in blk.tiles:
                    pb = PS.next()
                    for hh in range(8):
                        P.mm(pb[:, :n], w[:, hh, :], oT[:, hh, lc0:lc0 + n].k(("oT", hh, lc0)), start=hh == 0, stop=hh == 7)
                    P.copy("act", ob[:, d, lc0:lc0 + n].k(("ob", d, lc0)), pb[:, :n])
            postnorm_add(l, 1, blk)

    def ssd_block(l, j, blk):
        EW = ew_eng
        with Scope() as sc:
            hT = stateA.re("p (h q) -> p h q", h=32)
            hTb = stateB.re("p (h q) -> p h q", h=32)
            if blk.first:
                P.memset("dve", stateA, 0.0); P.memset("dve", stateB, 0.0)
                P.memset("dve", hist, 0.0); P.memset("dve", histb, 0.0)
            prenorm(l, 0, blk)
            ncb = len(blk.chunks)
            wdt = sc.sb("wdt", [128, 8, 32], BF16)
            P.dma("pool", wdt.re("p a b -> p (a b)"), sWdt[j])
            dtv = sc.sb("dtv", [128, ncb, 32]); av = sc.sb("av", [128, ncb, 32]); wts = sc.sb("wts", [128, ncb, 32])
            negA = sc.sb("negA", [128, 32])
            ta = sc.sb("ta", [128, 32]); tb = sc.sb("tb", [128, 32])
            P.act(negA, sAd[:, j, 32:64], AF.Exp, bias=zeroc)
            P.ts("dve", negA, negA, -1.0, MULT)
            for ci, (lc0, C, case) in enumerate(blk.chunks):
                pb = PS.next()
                for kc in range(8):
                    P.mm(pb[:C, 0:32], hn_t(kc, lc0, C, blk), wdt[:, kc, :], start=kc == 0, stop=kc == 7)
                P.tt("dve", tb[:C, :], pb[:C, 0:32], sAd[:C, j, 0:32], ADD)
                softplus(dtv[:C, ci, :], tb[:C, :], C, 32, ta, None)
                P.tt("dve", av[:C, ci, :], dtv[:C, ci, :], negA[:C, :], MULT)
                pg = PS.next()
                P.mm(pg[:C, 0:32], M(case, 0, C), av[:C, ci, :])
                P.mm(pg[:C, 32:64], M(case, 3, C), av[:C, ci, :])
                P.copy("dve", ta[:C, :], pg[:C, 0:32])
                P.tt("dve", tb[:C, :], pg[:C, 32:64], ta[:C, :], SUB)
                P.act(wts[:C, ci, :], tb[:C, :], AF.Exp, bias=zeroc[:C])

            yz = sc.sb("yz", [128, 16, NB], BF16)
            zs = sc.sb("zs", [128, 4, NB], BF16); xsT = sc.sb("xsT", [128, 4, NB], BF16)
            BT = sc.sb("BT", [128, NB], BF16); CT = sc.sb("CT", [128, NB], BF16)
            RAW = Ring("raw", [128, 520], BF16, 2, alloc=sc.sb)
            DG = Ring("dg", [128, 4, 128], BF16, 2, alloc=sc.sb)
            ACC = Ring("acc", [128, 512], F32, 2, alloc=sc.sb)
            HS = Ring("hs", [48, 128], F32, 2, alloc=sc.sb)
            stg = sc.sb("stg", [128, 320])
            TF = Ring("tf", [128, 128], F32, 4, alloc=sc.sb)
            A4 = Ring("a4", [128, 4, 128], F32, 3, alloc=sc.sb)
            AU = Ring("au", [128, 4, 128], F32, 2, alloc=sc.sb)
            B4 = Ring("b4", [128, 4, 128], BF16, 3, alloc=sc.sb)
            XD = Ring("xd", [128, 512], BF16, 4, alloc=sc.sb)
            Elast = sc.sb("Elast", [128, 8])
            BtokD = sc.sb("BtokD", [128, 128], BF16); CBmD = sc.sb("CBmD", [128, 128])
            masks_r = sc.sb("masks_r", [128, 2, 4, 128], F32R); ones_r = sc.sb("ones_r", [128, 128], F32R)
            P.copy("dve", masks_r, masks); P.copy("dve", ones_r, ones_f)
            dD = sc.sb("dD", [128, 16, 128], BF16)
            for t_ in range(16):
                P.ts("dve", dD[:, t_, :], ident, sD[:, j, t_:t_ + 1], MULT)
            if blk.last:
                H0 = Ring("h0", [128, 128], F32, 16, alloc=sc.sb)
                HTS = Ring("hts", [128, 512], BF16, 2, alloc=sc.sb)
                Bm = sc.sb("Bm", [64, 16, 128], BF16)
                decn = sc.sb("decn", [128, 4, 16])

            def tk(buf3, name, t, lc0, C):
                for (t0, tn, _) in blk.tiles:
                    if t0 <= lc0 < t0 + tn:
                        if t is None:
                            return V(buf3.ap[:, lc0:lc0 + C], [(name, t0)])
                        return V(buf3.ap[:, t, lc0:lc0 + C], [(name, t, t0)])

            for g in range(4):
                fts = [("z", g * 4 + q, q) for q in range(4)] + [("x", 16 + g * 4 + q, q) for q in range(4)] + \
                      [("B", 32 + g, 0), ("C", 36 + g, 0)]
                for (kind, wt, q) in fts:
                    w = load_w(sWt[j, wt])
                    cft = wt - 16
                    if kind != "z":
                        dg = build_diag(scw[:, j, cft, :], DG)
                    for (lc0, n, is_s) in blk.tiles:
                        pb = PS.next()
                        for kc in range(8):
                            P.mm(pb[:, :n], w[:, kc, :], hn_t(kc, lc0, n, blk), start=kc == 0, stop=kc == 7)
                        if kind == "z":
                            P.act(zs[:, q, lc0:lc0 + n].k(("zs", q, lc0)), pb[:, :n], AF.Silu, bias=zeroc)
                            continue
                        raw = RAW.next()
                        pc = conv_tile_pe(pb, n, is_s, dg, cft, raw, samp_hist(sC[j], cft, HS))
                        if blk.last and is_s:
                            conv_state_out_pe(o_sC[j], o_ssC[j], cft, pb, stg)
                        if kind == "x":
                            dst = xsT[:, q, lc0:lc0 + n].k(("xsT", q, lc0))
                        elif kind == "B":
                            dst = BT[:, lc0:lc0 + n].k(("BT", lc0))
                        else:
                            dst = CT[:, lc0:lc0 + n].k(("CT", lc0))
                        P.act(dst, pc, AF.Silu, bias=scb[:, j, cft:cft + 1])

                for ci, (lc0, C, case) in enumerate(blk.chunks):
                    samp = case == 1
                    Bc = tk(BT, "BT", None, lc0, C); Cc = tk(CT, "CT", None, lc0, C)
                    pT = PS.next().bitcast(BF16)
                    for q in range(4):
                        P.tr(pT[:C, q * 128:(q + 1) * 128], tk(xsT, "xsT", q, lc0, C), identb)
                    P.tr(pT[:C, 512:640], Bc, identb)
                    xdt = XD.next(); xdtw = XD.next()
                    P.tt("dve", xdt[:C, :].re("p (h q) -> p h q", h=8), pT[:C, 0:512].re("p (h q) -> p h q", h=8),
                         dtv[:C, ci, g * 8:g * 8 + 8].re("p (h o) -> p h o", o=1).bc([C, 8, 64]), MULT)
                    P.tt(EW, xdtw[:C, :].re("p (h q) -> p h q", h=8), xdt[:C, :].re("p (h q) -> p h q", h=8),
                         wts[:C, ci, g * 8:g * 8 + 8].re("p (h o) -> p h o", o=1).bc([C, 8, 64]), MULT)
                    Btok = BtokD
                    P.copy("act", Btok[:C, :], pT[:C, 512:640])
                    pC = PS.next()
                    P.mm(pC[:C, 0:C], Bc, Cc)
                    CBm = CBmD
                    P.tt("dve", CBm[:C, :C], pC[:C, 0:C], M(case, 0, C), MULT)
                    pY = PY
                    if samp:
                        P.tt("dve", Bm, Btok[:C, :].re("p (o c) -> p o c", o=1).bc([64, 16, 128]),
                             bm.re("p (s o) -> p s o", o=1).bc([64, 16, 128]), MULT)
                    for hb in range(2):
                        h0i = g * 8 + 4 * hb
                        aU4 = AU.next()
                        P.tt(EW, aU4.bitcast(F32R)[:C, :, :C], M(case, 0, C).re("p (o c) -> p o c", o=1).bc([C, 4, C]),
                             av[:C, ci, h0i:h0i + 4].re("p (h o) -> p h o", o=1).bc([C, 4, C]), MULT)
                        pA = PS.next(); pBc = PS.next()
                        pAv = pA.re("p (h c) -> p h c", h=4)
                        pBv = pBc.re("p (h c) -> p h c", h=4)
                        P.mm(pAv[:C, :, :C], masks_r[:C, case, 1, :C], aU4.bitcast(F32R)[:C, :, :C])
                        P.mm(pBv[:, :, :C], ones_r[:C, :], aU4.bitcast(F32R)[:C, :, :C])
                        Wt4 = A4.next()
                        P.act(Wt4[:C, :, :C], pAv[:C, :, :C], AF.Exp, bias=zeroc[:C])
                        W24 = B4.next()
                        P.tt(EW, W24[:C, :, :C], Wt4[:C, :, :C], CBm[:C, :C].re("p (o c) -> p o c", o=1).bc([C, 4, C]), MULT)
                        EAb4 = A4.next()
                        P.act(EAb4[:, :, :C], pBv[:, :, :C], AF.Exp, bias=zeroc)
                        Cd4 = B4.next()
                        P.tt(EW, Cd4[:, :, :C], Cc.re("p (o c) -> p o c", o=1).bc([128, 4, C]), EAb4[:, :, :C], MULT)
                        if not samp:
                            P.copy("act", Elast[:, 4 * hb:4 * hb + 4], EAb4[:, :, C - 1])
                            for q in range(4):
                                hg = 4 * hb + q
                                t = hg // 2; po = (hg % 2) * 64
                                yo = pY[po:po + 64, t * 128:t * 128 + C]
                                P.mm(yo, xdt[:C, hg * 64:(hg + 1) * 64], W24[:C, q, :C], start=True, stop=False)
                                P.mm(yo, hTb[:, g * 8 + hg, :].k(("hTb", g)), Cd4[:, q, :C], start=False, stop=False)
                                if q % 2 == 1:
                                    P.mm(pY[:, t * 128:t * 128 + C], dD[:, g * 4 + t, :], tk(xsT, "xsT", t, lc0, C),
                                         start=False, stop=True)
                        else:
                            for q in range(4):
                                hg = 4 * hb + q
                                t = hg // 2; po = (hg % 2) * 64
                                P.copy("act", decn[po:po + 64, t, :], EAb4[po:po + 64, q, 3:64:4])
                            its = [(t, s4) for t in (2 * hb, 2 * hb + 1) for s4 in range(4)]

                            def emit_in(t, s4):
                                hs = []
                                for q in range(4):
                                    s_ = 4 * s4 + q
                                    h0 = H0.next(); hs.append(h0)
                                    P.dma("sp", h0, sH[j, s_, g * 8 + 2 * t:g * 8 + 2 * t + 2].re("a q s -> (a q) s"))
                                return hs
                            pend = [emit_in(*its[0]), emit_in(*its[1])]
                            for ii, (t, s4) in enumerate(its):
                                if s4 == 0:
                                    for a_ in range(2):
                                        hg = 2 * t + a_
                                        P.mm(pY[a_ * 64:(a_ + 1) * 64, t * 128:t * 128 + C], xdt[:C, hg * 64:(hg + 1) * 64],
                                             W24[:C, hg - 4 * hb, :C], start=True, stop=False)
                                if ii + 2 < len(its):
                                    pend.append(emit_in(*its[ii + 2]))
                                h0s = pend[ii]
                                pX = PS.next()
                                for q in range(4):
                                    P.tr(pX[:, q * 128:(q + 1) * 128], h0s[q], ident)
                                hts = HTS.next()
                                P.copy("act", hts, pX)
                                for q in range(4):
                                    s_ = 4 * s4 + q
                                    for a_ in range(2):
                                        P.mm(pY[a_ * 64:(a_ + 1) * 64, t * 128 + 4 * s_:t * 128 + 4 * s_ + 4],
                                             hts[:, q * 128 + a_ * 64:q * 128 + (a_ + 1) * 64],
                                             Cd4[:, (t % 2) * 2 + a_, 4 * s_:4 * s_ + 4], start=False, stop=False)
                                pN = PS.next()
                                for q in range(4):
                                    P.mm(pN[:, q * 128:(q + 1) * 128], xdtw[:C, t * 128:(t + 1) * 128], Bm[:, 4 * s4 + q, :])
                                for q in range(4):
                                    s_ = 4 * s4 + q
                                    P.stt(h0s[q], h0s[q], decn[:, t, s_:s_ + 1], pN[:, q * 128:(q + 1) * 128], MULT, ADD)
                                    P.dma("sp", o_ssH[j, s_, g * 8 + 2 * t:g * 8 + 2 * t + 2].re("a q s -> (a q) s"), h0s[q],
                                          is_output=True)
                                if s4 == 3:
                                    P.mm(pY[:, t * 128:t * 128 + C], dD[:, g * 4 + t, :], tk(xsT, "xsT", t, lc0, C),
                                         start=False, stop=True)
                    if not samp:
                        pH = PS.next()
                        P.mm(pH[:, :], Btok[:C, :], xdtw[:C, :])
                        hg_ = hT[:, g * 8:g * 8 + 8, :].k(("hT", g))
                        P.tt(EW, hg_, hg_, Elast.re("p (h o) -> p h o", o=1).bc([128, 8, 64]), MULT)
                        P.tt("dve", hg_, hg_, pH.re("p (h q) -> p h q", h=8), ADD)
                        P.copy("act", hTb[:, g * 8:g * 8 + 8, :].k(("hTb", g)), hg_)
                    zk = [k for t in range(4) for k in tk(zs, "zs", t, lc0, C).keys]
                    P.tt("dve", V(yz.ap[:, g * 4:g * 4 + 4, lc0:lc0 + C], [("yz", g * 4 + t, lc0) for t in range(4)]),
                         pY.re("p (t c) -> p t c", t=4)[:, :, :C], V(zs.ap[:, :, lc0:lc0 + C], zk), MULT)
                if blk.last:
                    pX = PS.next()
                    for t in range(4):
                        P.tr(pX[:, t * 128:(t + 1) * 128], hT[:, g * 8 + 2 * t:g * 8 + 2 * t + 2, :].re("p a q -> p (a q)").k(("hT", g)), ident)
                    hn_out = ACC.next()
                    P.copy("act", hn_out, pX)
                    P.dma("sp", o_sH[j, g * 8:g * 8 + 8].re("(t a) q s -> (a q) t s", a=2),
                          hn_out.re("p (t s) -> p t s", t=4), is_output=True)
            for (lc0, n, _) in blk.tiles:
                pb = PS.next()
                for t in range(16):
                    sq = SQ.next()
                    yk = V(yz.ap[:, t, lc0:lc0 + n], [("yz", t, lc0 + i * 128) for i in range((n + 127) // 128)])
                    P.act(sq[:, :n], yk, AF.Square)
                    P.mm(pb[:, :n], onesb, sq[:, :n], start=t == 0, stop=t == 15)
                rs = RS.next()
                rstd_from_ps(pb, n, 1.0 / 2048, rs)
                for t in range(16):
                    yk = V(yz.ap[:, t, lc0:lc0 + n], [("yz", t, lc0 + i * 128) for i in range((n + 127) // 128)])
                    P.stt(yk, yk, snw[:, j, t:t + 1], rs[:, :n], MULT, MULT)
            for d in range(8):
                w0 = load_w(sWo[j, d, :, 0:1024]); w1 = load_w(sWo[j, d, :, 1024:2048])
                for (lc0, n, _) in blk.tiles:
                    pb = PS.next()
                    for t in range(16):
                        w = w0 if t < 8 else w1
                        yk = V(yz.ap[:, t, lc0:lc0 + n], [("yz", t, lc0 + i * 128) for i in range((n + 127) // 128)])
                        P.mm(pb[:, :n], w[:, t % 8, :], yk, start=t == 0, stop=t == 15)
                    P.copy("act", ob[:, d, lc0:lc0 + n].k(("ob", d, lc0)), pb[:, :n])
            postnorm_add(l, 1, blk)

    def ffn_block(l, blk):
        P.mark("ffn start")
        with Scope() as sc:
            prenorm(l, 2, blk)
            act = sc.sb("act", [128, 22, NB], BF16)
            WD = Ring("wd", [128, 22, 128], BF16, 2, alloc=sc.sb)
            GT = Ring("gt", [128, 512], F32, 2, alloc=sc.sb)
            for f in range(22):
                wg = load_w(fWg[l, f]); wu = load_w(fWu[l, f])
                for (lc0, n, _) in blk.tiles:
                    pg = PS.next(); pu = PS.next()
                    for kc in range(8):
                        P.mm(pg[:, :n], wg[:, kc, :], hn_t(kc, lc0, n, blk), start=kc == 0, stop=kc == 7)
                    for kc in range(8):
                        P.mm(pu[:, :n], wu[:, kc, :], hn_t(kc, lc0, n, blk), start=kc == 0, stop=kc == 7)
                    gt = GT.next()
                    P.act(gt[:, :n], pg[:, :n], AF.Silu, bias=zeroc)
                    P.tt("dve", act[:, f, lc0:lc0 + n].k(("act", f, lc0)), gt[:, :n], pu[:, :n], MULT)
            for d in range(8):
                wd = WD.next()
                P.dma("pool", wd[:, 0:11, :].re("p a b -> p (a b)"), fWd[l, d, :, 0:1408])
                P.dma("pool", wd[:, 11:22, :].re("p a b -> p (a b)"), fWd[l, d, :, 1408:2816])
                for (lc0, n, _) in blk.tiles:
                    pb = PS.next()
                    for f in range(22):
                        P.mm(pb[:, :n], wd[:, f, :], act[:, f, lc0:lc0 + n].k(("act", f, lc0)), start=f == 0, stop=f == 21)
                    P.copy("act", ob[:, d, lc0:lc0 + n].k(("ob", d, lc0)), pb[:, :n])
            postnorm_add(l, 3, blk)

    for l in range(depth):
        for blk in blocks:
            if l % 2 == 0:
                gdn_block(l, l // 2, blk)
            else:
                ssd_block(l, l // 2, blk)
            ffn_block(l, blk)

    with Scope() as sc9:
        YO = Ring("yo", [128, 1024], F32, 2, alloc=sc9.sb)
        for g in range((T + 127) // 128):
            c0 = g * 128
            n = min(128, T - c0)
            t = YO.next()
            for half in range(2):
                pb = PS.next()
                for q in range(4):
                    dt = half * 4 + q
                    P.tr(pb[:n, q * 128:(q + 1) * 128], V(x.ap[:, dt, c0:c0 + n], [("x", g)]), ident)
                P.copy("act" if half else "dve", t[:n, half * 512:(half + 1) * 512], pb[:n, :])
            if c0 < L:
                P.dma("sp", yp[c0:c0 + n, :], t[:n, :], is_output=True)
            else:
                P.dma("sp", ys[c0 - L:c0 - L + n, :], t[:n, :], is_output=True)
    P.finish()
    return nc, P


def _tile_w(w):
    K, F = w.shape
    kc, ft = K // 128, F // 128
    return np.ascontiguousarray(w.reshape(kc, 128, ft, 128).transpose(2, 1, 0, 3)).reshape(ft, 128, kc * 128)


def _feat(v):
    F = v.shape[-1]
    r = v.reshape(v.shape[:-1] + (F // 128, 128))
    return np.ascontiguousarray(np.moveaxis(r, -1, 0))


def _consts():
    m = np.zeros((128, 2, 4, 128), np.float32)
    for case, (C, bs) in enumerate([(128, 128), (64, 4)]):
        idx = np.arange(C)
        same = (idx[:, None] // bs) == (idx[None, :] // bs)
        m[:C, case, 0, :C] = (idx[:, None] <= idx[None, :]) & same
        m[:C, case, 1, :C] = (idx[:, None] > idx[None, :]) & same
        m[:C, case, 2, :C] = (idx[None, :] > idx[:, None]) & same
        m[:C, case, 3, :C] = same
    i64 = np.arange(64)
    cm = np.broadcast_to(((i64[None, :] // 4) == np.arange(16)[:, None])[None], (128, 16, 64)).astype(np.float32)
    bm = ((i64[:, None] // 4) == np.arange(16)[None, :]).astype(np.float32)
    return dict(ident=np.eye(128, dtype=np.float32), masks=m.reshape(128, 1024),
                cm=np.ascontiguousarray(cm).reshape(128, 1024), bm=bm)


_CACHE = {}


def _shared_inputs(I):
    f = np.float32
    d = {}
    nw = np.stack([_feat(np.asarray(I[k], f)) for k in ("mix_pre_norm", "mix_post_norm", "ffn_pre_norm", "ffn_post_norm")], 2)
    d["nw"] = np.ascontiguousarray(nw).reshape(128, -1)
    gw = np.asarray(I["gdn_w_in"], f)
    d["gWt"] = np.stack([_tile_w(gw[l][:, :4096]) for l in range(2)])
    d["gWba"] = np.stack([np.ascontiguousarray(gw[l][:, 4096:4112].reshape(8, 128, 16).transpose(1, 0, 2)).reshape(128, 128) for l in range(2)])
    d["gcw"] = np.ascontiguousarray(np.asarray(I["gdn_conv_w"], f).reshape(2, 4, 24, 128).transpose(3, 0, 2, 1)).reshape(128, -1)
    gad = np.concatenate([np.asarray(I["gdn_A_log"], f), np.asarray(I["gdn_dt_bias"], f)], -1)
    d["gAd"] = np.ascontiguousarray(np.broadcast_to(gad[None], (128, 2, 16))).reshape(128, 32)
    d["gnw"] = np.ascontiguousarray(np.asarray(I["gdn_norm_w"], f).T)
    d["gWo"] = np.stack([_tile_w(np.asarray(I["gdn_w_out"], f)[l]) for l in range(2)])
    sw = np.asarray(I["ssd_w_in"], f)
    d["sWt"] = np.stack([_tile_w(sw[l][:, :5120]) for l in range(2)])
    d["sWdt"] = np.stack([np.ascontiguousarray(sw[l][:, 5120:5152].reshape(8, 128, 32).transpose(1, 0, 2)).reshape(128, 256) for l in range(2)])
    d["scw"] = np.ascontiguousarray(np.asarray(I["ssd_conv_w"], f).reshape(2, 4, 24, 128).transpose(3, 0, 2, 1)).reshape(128, -1)
    d["scb"] = np.ascontiguousarray(np.asarray(I["ssd_conv_b"], f).reshape(2, 24, 128).transpose(2, 0, 1)).reshape(128, 48)
    sad = np.concatenate([np.asarray(I["ssd_dt_bias"], f), np.asarray(I["ssd_A_log"], f)], -1)
    d["sAd"] = np.ascontiguousarray(np.broadcast_to(sad[None], (128, 2, 64))).reshape(128, 128)
    D = np.asarray(I["ssd_D"], f)
    hd = 2 * np.arange(16)[None, :] + (np.arange(128)[:, None] // 64)
    d["sD"] = np.ascontiguousarray(np.stack([D[l][hd] for l in range(2)], 1)).reshape(128, 32)
    d["snw"] = np.ascontiguousarray(np.asarray(I["ssd_norm_w"], f).reshape(2, 16, 128).transpose(2, 0, 1)).reshape(128, 32)
    d["sWo"] = np.stack([_tile_w(np.asarray(I["ssd_w_out"], f)[l]) for l in range(2)])
    d["fWg"] = np.stack([_tile_w(np.asarray(I["ffn_w_gate"], f)[l]) for l in range(4)])
    d["fWu"] = np.stack([_tile_w(np.asarray(I["ffn_w_up"], f)[l]) for l in range(4)])
    d["fWd"] = np.stack([_tile_w(np.asarray(I["ffn_w_down"], f)[l]) for l in range(4)])
    d.update(_consts())
    return d


def kernel(_L=None, _depth=4, _cores=8, _cpb=None, _inv="f32r", **I):
    f = np.float32
    xp = np.asarray(I["x_prompt"], f)
    L = _L or xp.shape[1]
    key = (L, _depth, tuple(_cpb) if _cpb else None, _inv)
    if key not in _CACHE:
        _CACHE[key] = build(L=L, depth=_depth, chunks_per_block=_cpb, inv_dt=_inv)[0]
    nc = _CACHE[key]
    shared = _shared_inputs(I)
    xs = np.asarray(I["x_sample"], f)
    gS = np.asarray(I["state_gdn_S"], f); gC = np.asarray(I["state_gdn_conv"], f)
    sH = np.asarray(I["state_ssd_h"], f); sC = np.asarray(I["state_ssd_conv"], f)
    in_maps = []
    for c in range(_cores):
        m = dict(shared)
        m["xp"] = np.ascontiguousarray(xp[c, :L])
        sl = slice(16 * c, 16 * c + 16)
        m["xs"] = np.ascontiguousarray(xs[sl]).reshape(64, 1024)
        m["gS"] = np.ascontiguousarray(gS[:, sl])
        m["gC"] = np.ascontiguousarray(gC[:, sl]).reshape(2, 48, 3072)
        m["sH"] = np.ascontiguousarray(sH[:, sl])
        m["sC"] = np.ascontiguousarray(sC[:, sl]).reshape(2, 48, 3072)
        in_maps.append(m)
    res = run_bass_kernel_spmd(nc, in_maps, core_ids=list(range(_cores)))
    R = res.results
    cat = lambda k, ax: np.concatenate([np.asarray(r[k], f) for r in R], axis=ax)
    y_prompt = np.stack([np.asarray(r["yp"], f) for r in R], 0)
    y_sample = cat("ys", 0).reshape(16 * _cores, 4, 1024)
    p_gS = np.stack([np.asarray(r["o_gS"], f) for r in R], 1)
    p_gC = np.stack([np.asarray(r["o_gC"], f) for r in R], 1)
    p_sH = np.stack([np.asarray(r["o_sH"], f) for r in R], 1)
    p_sC = np.stack([np.asarray(r["o_sC"], f) for r in R], 1)
    s_gS = cat("o_sgS", 1)
    s_gC = np.concatenate([np.asarray(r["o_sgC"], f).reshape(2, 16, 3, 3072) for r in R], 1)
    s_sH = cat("o_ssH", 1)
    s_sC = np.concatenate([np.asarray(r["o_ssC"], f).reshape(2, 16, 3, 3072) for r in R], 1)
    return (y_prompt, y_sample, p_gS, p_gC, p_sH, p_sC, s_gS, s_gC, s_sH, s_sC)
```
